# Optimizing a Trainium2 kernel written in Bass

```python
import math
import jax, jax.numpy as jnp
from jax import lax
import numpy as np

D_MODEL = 1024
BATCH = 4
SEQ = 8192
DEPTH = 1

HEAD_DIM = 64
A_HEADS = 8
A_KV_HEADS = 2
A_GROUP = A_HEADS // A_KV_HEADS
B_HEADS = 8
A_Q_W = A_HEADS * HEAD_DIM
A_KV_W = A_KV_HEADS * HEAD_DIM
B_W = B_HEADS * HEAD_DIM
MIX_W = A_Q_W + B_W
D_IN = A_Q_W + 2 * A_KV_W + 3 * B_W
WINDOW = 128
BLK = 128
ROPE_THETA = 500000.0
ROT_DIM = HEAD_DIM // 4
GRID_W = 64
NA_KH = 8
NA_KW = 16
D_FF = 2816
CONV_W = 3
EPS = 1e-6

kernel_name = "hybrid_window_gqa_neighbourhood_convffn"


def rmsnorm(x, g):
    xf = x.astype(jnp.float32)
    y = xf * lax.rsqrt(jnp.mean(xf * xf, axis=-1, keepdims=True) + EPS)
    return (y * g.astype(jnp.float32)).astype(x.dtype)


def partial_rope(t, cos, sin):
    half = ROT_DIM // 2
    x1 = t[..., :half]
    x2 = t[..., half:ROT_DIM]
    return jnp.concatenate([x1 * cos - x2 * sin, x2 * cos + x1 * sin, t[..., ROT_DIM:]], axis=-1)


def window_gqa_sink(q, k, v, sink):
    B, S = q.shape[0], q.shape[1]
    nb = S // BLK
    span = BLK + 2 * WINDOW
    scale = 1.0 / math.sqrt(HEAD_DIM)
    qb = q.reshape(B, nb, BLK, A_KV_HEADS, A_GROUP, HEAD_DIM)
    qb = jnp.moveaxis(qb, 1, 0)
    pad = ((0, 0), (WINDOW, WINDOW), (0, 0), (0, 0))
    kp = jnp.pad(k, pad)
    vp = jnp.pad(v, pad)
    sink_gr = sink.astype(jnp.float32).reshape(A_KV_HEADS, A_GROUP)[None, :, :, None]

    def block(args):
        q_i, i = args
        start = i * BLK
        k_i = lax.dynamic_slice_in_dim(kp, start, span, axis=1)
        v_i = lax.dynamic_slice_in_dim(vp, start, span, axis=1)
        s = jnp.einsum('bqgrd,bkgd->bgrqk', q_i, k_i).astype(jnp.float32) * scale
        qpos = i * BLK + jnp.arange(BLK)
        kpos = i * BLK - WINDOW + jnp.arange(span)
        valid = (jnp.abs(qpos[:, None] - kpos[None, :]) <= WINDOW) & (kpos[None, :] >= 0) & (kpos[None, :] < S)
        s = jnp.where(valid[None, None, None], s, -1e30)
        m = jnp.maximum(jnp.max(s, axis=-1), sink_gr)
        p = jnp.exp(s - m[..., None])
        denom = jnp.sum(p, axis=-1) + jnp.exp(sink_gr - m)
        o = jnp.einsum('bgrqk,bkgd->bgrqd', p.astype(v_i.dtype), v_i)
        o = o / denom[..., None].astype(o.dtype)
        return jnp.transpose(o, (0, 3, 1, 2, 4))

    out = lax.map(block, (qb, jnp.arange(nb, dtype=jnp.int32)))
    out = jnp.moveaxis(out, 0, 1)
    return out.reshape(B, S, A_Q_W)


def neighbourhood_attention(q, k, v, rpb):
    B, S = q.shape[0], q.shape[1]
    rows = S // GRID_W
    kh = min(NA_KH, rows)
    scale = 1.0 / math.sqrt(HEAD_DIM)
    cs = np.clip(np.arange(GRID_W) - NA_KW // 2, 0, GRID_W - NA_KW)
    col_idx = cs[:, None] + np.arange(NA_KW)[None, :]
    col_off = col_idx - np.arange(GRID_W)[:, None] + (NA_KW - 1)
    qg = jnp.moveaxis(q.reshape(B, rows, GRID_W, B_HEADS, HEAD_DIM), 1, 0)
    kg = k.reshape(B, rows, GRID_W, B_HEADS, HEAD_DIM)
    vg = v.reshape(B, rows, GRID_W, B_HEADS, HEAD_DIM)
    rpb32 = rpb.astype(jnp.float32)

    def row(args):
        q_r, r = args
        rs = jnp.clip(r - kh // 2, 0, rows - kh)
        k_rows = lax.dynamic_slice_in_dim(kg, rs, kh, axis=1)
        v_rows = lax.dynamic_slice_in_dim(vg, rs, kh, axis=1)
        k_win = k_rows[:, :, col_idx]
        v_win = v_rows[:, :, col_idx]
        row_off = rs + jnp.arange(kh) - r + (NA_KH - 1)
        bias = rpb32[:, row_off][:, :, col_off]
        bias = jnp.transpose(bias, (0, 2, 1, 3))
        s = jnp.einsum('bchd,bicjhd->bhcij', q_r, k_win).astype(jnp.float32) * scale + bias[None]
        p = jax.nn.softmax(s.reshape(B, B_HEADS, GRID_W, kh * NA_KW), axis=-1)
        p = p.reshape(B, B_HEADS, GRID_W, kh, NA_KW).astype(v_win.dtype)
        return jnp.einsum('bhcij,bicjhd->bchd', p, v_win)

    out = lax.map(row, (qg, jnp.arange(rows, dtype=jnp.int32)))
    out = jnp.moveaxis(out, 0, 1)
    return out.reshape(B, S, B_W)


def conv_gated_mlp(h, w_gate, w_up, conv_w, conv_b, w_down):
    g = jnp.einsum('bsd,df->bsf', h, w_gate)
    u = jnp.einsum('bsd,df->bsf', h, w_up)
    g = lax.conv_general_dilated(
        g, conv_w[:, None, :], window_strides=(1,),
        padding=((CONV_W // 2, CONV_W // 2),),
        dimension_numbers=('NWC', 'WIO', 'NWC'),
        feature_group_count=D_FF) + conv_b
    return jnp.einsum('bsf,fd->bsd', jax.nn.silu(g) * u, w_down)


def setup_inputs(seed: int = 0) -> dict:
    key = jax.random.key(seed)
    ks = jax.random.split(key, 16)
    f32 = jnp.float32
    nrm = lambda k, shape, s: jax.random.normal(k, shape, f32) * s
    return {
        "x": jax.random.normal(ks[0], (BATCH, SEQ, D_MODEL), f32),
        "norm_mix": 1.0 + nrm(ks[1], (DEPTH, D_MODEL), 0.02),
        "w_in": nrm(ks[2], (DEPTH, D_MODEL, D_IN), D_MODEL ** -0.5),
        "sink": nrm(ks[3], (DEPTH, A_HEADS), 0.5),
        "rpb": nrm(ks[4], (DEPTH, B_HEADS, 2 * NA_KH - 1, 2 * NA_KW - 1), 0.1),
        "norm_a": 1.0 + nrm(ks[5], (DEPTH, A_Q_W), 0.02),
        "norm_b": 1.0 + nrm(ks[6], (DEPTH, B_W), 0.02),
        "w_out": nrm(ks[7], (DEPTH, MIX_W, D_MODEL), MIX_W ** -0.5),
        "norm_ffn": 1.0 + nrm(ks[8], (DEPTH, D_MODEL), 0.02),
        "w_gate": nrm(ks[9], (DEPTH, D_MODEL, D_FF), D_MODEL ** -0.5),
        "w_up": nrm(ks[10], (DEPTH, D_MODEL, D_FF), D_MODEL ** -0.5),
        "conv_w": nrm(ks[11], (DEPTH, CONV_W, D_FF), CONV_W ** -0.5),
        "conv_b": nrm(ks[12], (DEPTH, D_FF), 0.02),
        "w_down": nrm(ks[13], (DEPTH, D_FF, D_MODEL), D_FF ** -0.5),
        "norm_final": 1.0 + nrm(ks[14], (D_MODEL,), 0.02),
    }


def reference(x, norm_mix, w_in, sink, rpb, norm_a, norm_b, w_out, norm_ffn,
              w_gate, w_up, conv_w, conv_b, w_down, norm_final):
    B, S = x.shape[0], x.shape[1]
    pos = jnp.arange(S, dtype=jnp.float32)
    inv_freq = ROPE_THETA ** (-jnp.arange(0, ROT_DIM, 2, dtype=jnp.float32) / ROT_DIM)
    ang = pos[:, None] * inv_freq[None, :]
    cos = jnp.cos(ang)[:, None, :].astype(x.dtype)
    sin = jnp.sin(ang)[:, None, :].astype(x.dtype)
    splits = np.cumsum([A_Q_W, A_KV_W, A_KV_W, B_W, B_W]).tolist()
    h = x
    for l in range(DEPTH):
        hn = rmsnorm(h, norm_mix[l])
        proj = jnp.einsum('bsd,de->bse', hn, w_in[l])
        qa, ka, va, qb, kb, vb = jnp.split(proj, splits, axis=-1)
        qa = partial_rope(qa.reshape(B, S, A_HEADS, HEAD_DIM), cos, sin)
        ka = partial_rope(ka.reshape(B, S, A_KV_HEADS, HEAD_DIM), cos, sin)
        va = va.reshape(B, S, A_KV_HEADS, HEAD_DIM)
        o_a = window_gqa_sink(qa, ka, va, sink[l])
        o_b = neighbourhood_attention(
            qb.reshape(B, S, B_HEADS, HEAD_DIM),
            kb.reshape(B, S, B_HEADS, HEAD_DIM),
            vb.reshape(B, S, B_HEADS, HEAD_DIM), rpb[l])
        mix = jnp.concatenate([rmsnorm(o_a, norm_a[l]), rmsnorm(o_b, norm_b[l])], axis=-1)
        h = h + jnp.einsum('bse,ed->bsd', mix, w_out[l])
        hn = rmsnorm(h, norm_ffn[l])
        h = h + conv_gated_mlp(hn, w_gate[l], w_up[l], conv_w[l], conv_b[l], w_down[l])
    return rmsnorm(h, norm_final)
```

```python
import math
from contextlib import ExitStack
import numpy as np
import concourse.bass as bass
import concourse.mybir as mybir
from concourse.bass_utils import run_bass_kernel_spmd

F32 = mybir.dt.float32
BF16 = mybir.dt.bfloat16
ALU = mybir.AluOpType
AF = mybir.ActivationFunctionType

D = 1024
SEQ = 8192
NCORES = 8
NO_FULL = 32
DFF = 2816
NFC = 22
EPS = 1e-6
MASKV = -100.0
DJ_GEN = (-2, -1, 0, 1, 2)
DJ_TOP = (-2, -1, 0, 1, 2, 3)
DJ_BOT = (-3, -2, -1, 0, 1, 2)
NDSEM = 80
NDSEM_SP = 16


class Sched:
    ENG = ("pe", "act", "dve", "pool", "sp")

    def __init__(self):
        self.ops = {n: [] for n in self.ENG}
        self.cnt = {n: 0 for n in self.ENG}
        self.waited = {n: {} for n in self.ENG}
        self.bufs = {}
        self.dcnt = [0] * NDSEM
        self.dnext = {"sp": 0, "pool": 0}
        self.out_events = []

    def op(self, eng, fn, r=(), w=(), dma=False, extra=(), is_out=False):
        deps = {}

        def need(ev):
            if ev is not None:
                if deps.get(ev[0], 0) < ev[1]:
                    deps[ev[0]] = ev[1]

        for k in r:
            b = self.bufs.get(k)
            if b is not None:
                need(b[0])
                if isinstance(k, tuple) and k[0] == "bank":
                    for ev in b[1]:
                        if ev[0] != eng:
                            need(ev)
        for k in w:
            b = self.bufs.get(k)
            if b is not None:
                need(b[0])
                for ev in b[1]:
                    need(ev)
        for ev in extra:
            need(ev)
        if dma:
            lo_, n_ = (0, NDSEM_SP) if eng == "sp" else (NDSEM_SP, NDSEM - NDSEM_SP)
            k = lo_ + self.dnext[eng]
            self.dnext[eng] = (self.dnext[eng] + 1) % n_
            if self.dcnt[k] > 0:
                need((("d", k), self.dcnt[k]))
            self.dcnt[k] += 16
            ev = (("d", k), self.dcnt[k])
        else:
            self.cnt[eng] += 1
            ev = (eng, self.cnt[eng])
        waits = []
        for sk, v in deps.items():
            if sk == eng and eng == "pe":
                continue
            if self.waited[eng].get(sk, 0) >= v:
                continue
            self.waited[eng][sk] = v
            waits.append((sk, v))
        self.ops[eng].append((waits, fn, ev))
        for k in r:
            self.bufs.setdefault(k, [None, []])[1].append(ev)
        for k in w:
            self.bufs[k] = [ev, []]
        if is_out:
            self.out_events.append(ev)
        return ev

    def barrier_events(self):
        evs = [(n, self.cnt[n]) for n in self.ENG if self.cnt[n] > 0]
        evs += [(("d", k), self.dcnt[k]) for k in range(NDSEM) if self.dcnt[k] > 0]
        return evs


def build_program(NO=NO_FULL):
    NX, NE, OWN = NO + 6, NO + 2, NO * 128
    SPECIAL_E = (1, 2, NO - 1, NO)
    nc = bass.Bass("TRN2", target_bir_lowering=False)
    S = Sched()

    def din(name, shape):
        return nc.dram_tensor(name, list(shape), F32, kind="ExternalInput").ap()

    x_d = din("x", [NX * 128, D])
    w_in_d = din("w_in", [D, 2304])
    w_out_d = din("w_out", [D, D])
    w_gate_d = din("w_gate", [D, DFF])
    w_up_d = din("w_up", [D, DFF])
    w_down_d = din("w_down", [DFF, D])
    ident_d = din("ident", [128, 128])
    gains_d = din("gains", [128, 24])
    gfin_d = din("gfin", [D])
    cs2_d = din("cs2", [128, NX * 16])
    sn2_d = din("sn2", [128, NX * 16])
    maskA_d = din("maskA", [128, 4 * 128])
    biasG_d = din("biasG", [128, 8 * 5 * 128])
    biasS_d = din("biasS", [4, 128, 8 * 6 * 128])
    sinkb_d = din("sinkb", [128, 8])
    cw_d = din("cw", [128, NFC * 3])
    cb_d = din("cb", [128, NFC])
    hvalid_d = din("hvalid", [128, 2])
    out_d = nc.dram_tensor("out", [OWN, D], F32, kind="ExternalOutput").ap()
    h1d = nc.dram_tensor("h1scratch", [NE * 128, D], F32, kind="Internal").ap()

    es = ExitStack()

    def sb(name, shape, dt):
        return es.enter_context(nc.sbuf_tensor(name, list(shape), dt))

    ARENA = 67584
    arena = sb("arena", [128, ARENA], BF16)
    aoff = [0]

    def carve(n):
        o = aoff[0]
        aoff[0] += n
        assert aoff[0] <= ARENA, aoff[0]
        return arena[:, o:o + n]

    w_in_sb = carve(8 * 2304).rearrange("p (c e) -> p c e", c=8)
    w_out_sb = carve(8 * 1024).rearrange("p (c e) -> p c e", c=8)
    qT = carve(8 * 8 * 128).rearrange("p (b s t) -> p b s t", b=8, s=8)
    kaT = carve(8 * 128).rearrange("p (s t) -> p s t", s=8)
    kbT = carve(4 * 8 * 128).rearrange("p (b s t) -> p b s t", b=4, s=8)
    vA = carve(8 * 2 * 65).rearrange("p (s g d) -> p s g d", s=8, g=2)
    vB = carve(8 * 8 * 65).rearrange("p (s h d) -> p s h d", s=8, h=8)
    expbG = carve(8 * 5 * 128).rearrange("p (h j q) -> p h j q", h=8, j=5)
    expbS = carve(8 * 6 * 128).rearrange("p (h j q) -> p h j q", h=8, j=6)
    pA = [carve(3 * 512).rearrange("p (j q) -> p j q", j=3) for _ in range(2)]
    xn_bf = [carve(1024) for _ in range(2)]
    xnT = [carve(1024).rearrange("p (c t) -> p c t", c=8) for _ in range(2)]
    qk_tok = [carve(1664) for _ in range(2)]
    p1_end = aoff[0]
    aoff[0] = 0
    wg_sb = carve(8 * DFF).rearrange("p (c f) -> p c f", c=8)
    wu_sb = carve(8 * DFF).rearrange("p (c f) -> p c f", c=8)
    wd_sb = carve(NFC * 1024).rearrange("p (c e) -> p c e", c=NFC)

    ident_bf = sb("ident_bf", [128, 128], BF16)
    maskA_bf = sb("maskA_bf", [128, 4, 128], BF16)
    junk = sb("junk", [128, 1024], BF16)
    pB = [sb(f"pB{i}", [128, 768], BF16) for i in range(2)]
    mixT = sb("mixT", [128, 8, 128], BF16)
    hn2T = sb("hn2T", [128, 8, 8, 128], BF16)
    actc = [sb(f"actc{i}", [128, 256], BF16) for i in range(2)]

    big = [sb(f"big{i}", [128, 1024], F32) for i in range(5)]
    ropetmp = sb("ropetmp", [128, 160], F32)
    bstage = [sb("bstage0", [128, 768], F32)]
    gains = sb("gains_sb", [128, 24], F32)
    gfin = sb("gfin_sb", [128, 1024], F32)
    cs2 = sb("cs2_sb", [128, NX, 16], F32)
    sn2 = sb("sn2_sb", [128, NX, 16], F32)
    esink = sb("esink", [128, 8], F32)
    cw = sb("cw_sb", [128, NFC, 3], F32)
    cb = sb("cb_sb", [128, NFC], F32)
    hvalid = sb("hvalid_sb", [128, 2], F32)
    cm05 = sb("cm05", [128, 1], F32)
    stat = sb("stat", [128, 64], F32)
    ropeA = [sb(f"ropeA{i}", [128, 8, 16], F32) for i in range(2)]
    ropeK = [sb(f"ropeK{i}", [128, 2, 16], F32) for i in range(2)]
    den = sb("den", [128, 32], F32)
    accs = [sb(f"acc{i}", [128, 256], F32) for i in range(2)]
    sgs = [sb(f"sg{i}", [128, 256], F32) for i in range(2)]
    usb = [sb(f"usb{i}", [128, 256], F32) for i in range(2)]
    win = [sb("win0", [128, 8, 258], BF16)]

    psum = es.enter_context(nc.psum_tensor("psum", [128, 4096], F32))

    def bank(i, n=512, off=0):
        return psum[:, i * 512 + off:i * 512 + off + n]

    def bank_bf(i):
        return psum[:, i * 512:(i + 1) * 512].bitcast(BF16)

    sem_eng = {n: es.enter_context(nc.semaphore(f"s_{n}")) for n in Sched.ENG}
    sem_dma = [es.enter_context(nc.semaphore(f"s_d{k}")) for k in range(NDSEM)]

    g1 = gains[:, 0:8]
    g2 = gains[:, 8:16]
    gab = gains[:, 16:24]

    def st(i):
        return stat[:, i:i + 1]

    def dma(eng, out, in_, r=(), w=(), extra=(), is_out=False):
        return S.op(eng, lambda e: e.dma_start(out=out, in_=in_), r=r, w=w, dma=True,
                    extra=extra, is_out=is_out)

    def rstd_from_ss(ss_ap, ss_key, n, out_ap, out_key, tmp_ap, tmp_key):
        S.op("dve", lambda e: e.tensor_scalar(out=tmp_ap, in0=ss_ap, scalar1=1.0 / n, scalar2=EPS,
                                              op0=ALU.mult, op1=ALU.add), r=[ss_key], w=[tmp_key])
        S.op("pool", lambda e: e.tensor_tensor(out=out_ap, in0=tmp_ap, in1=cm05[:, 0:1], op=ALU.pow),
             r=[tmp_key, "cm05"], w=[out_key])

    def transposes8(src_bf, src_key, dst_bank):
        def fn(e):
            ins = None
            o = bank_bf(dst_bank)
            for c in range(8):
                ins = e.transpose(out=o[:, c * 128:(c + 1) * 128], in_=src_bf[:, c * 128:(c + 1) * 128],
                                  identity=ident_bf[:])
            return ins
        S.op("pe", fn, r=[src_key, "ident"], w=[("bank", dst_bank)])

    S.op("pool", lambda e: e.memset(cm05[:], -0.5), w=["cm05"])
    S.op("pool", lambda e: e.memset(vA.rearrange("p s g d -> p (s g d)"), 1.0),
         w=[("vA", s) for s in range(8)])
    S.op("pool", lambda e: e.memset(vB.rearrange("p s h d -> p (s h d)"), 1.0),
         w=[("vB", s) for s in range(8)])
    dma("pool", ident_bf[:], ident_d, w=["ident"])
    dma("pool", maskA_bf.rearrange("p a q -> p (a q)"), maskA_d, w=["maskA"])
    dma("sp", gains[:], gains_d, w=["gains"])
    dma("sp", cs2.rearrange("p j d -> p (j d)"), cs2_d, w=["cs2"])
    dma("sp", sn2.rearrange("p j d -> p (j d)"), sn2_d, w=["sn2"])
    dma("sp", esink[:], sinkb_d, w=["esink_raw"])
    S.op("act", lambda e: e.activation(out=esink[:], in_=esink[:], func=AF.Exp), r=["esink_raw"], w=["esink"])
    dma("sp", cw.rearrange("p c k -> p (c k)"), cw_d, w=["cw"])
    dma("sp", cb[:], cb_d, w=["cb"])
    dma("sp", hvalid[:], hvalid_d, w=["hvalid"])
    dma("sp", gfin[:], gfin_d.partition_broadcast(128), w=["gfin"])
    for c in range(8):
        dma("pool", w_in_sb[:, c, :], w_in_d[c * 128:(c + 1) * 128, :], w=["w_in"])
    for c in range(8):
        dma("pool", w_out_sb[:, c, :], w_out_d[c * 128:(c + 1) * 128, :], w=["w_out"])
    for h in range(8):
        bs = bstage[0]
        dma("sp", bs[:, 0:640], biasG_d[:, h * 640:(h + 1) * 640], w=[("bstage", 0)])
        S.op("act", (lambda h, bs: lambda e: e.activation(
            out=expbG[:, h, :, :].rearrange("p j q -> p (j q)"), in_=bs[:, 0:640], func=AF.Exp))(h, bs),
            r=[("bstage", 0)], w=[("expbG", h)])

    def rope(xv, nh, rp, krp, tp, ktp, kb, j):
        S.op("dve", lambda e: e.tensor_tensor(out=rp[:], in0=xv[:, :, 0:16],
                                              in1=cs2[:, j:j + 1, :].to_broadcast([128, nh, 16]), op=ALU.mult),
             r=[kb, "cs2"], w=[krp])
        S.op("dve", lambda e: e.tensor_tensor(out=tp[:, :, 0:8], in0=xv[:, :, 8:16],
                                              in1=sn2[:, j:j + 1, 0:8].to_broadcast([128, nh, 8]), op=ALU.mult),
             r=[kb, "sn2"], w=[ktp])
        S.op("dve", lambda e: e.tensor_tensor(out=tp[:, :, 8:16], in0=xv[:, :, 0:8],
                                              in1=sn2[:, j:j + 1, 8:16].to_broadcast([128, nh, 8]), op=ALU.mult),
             r=[kb, "sn2", ktp], w=[ktp])
        S.op("dve", lambda e: e.tensor_tensor(out=rp[:], in0=rp[:], in1=tp, op=ALU.add),
             r=[ktp, krp], w=[krp])

    IN_TILES = ((0, 512), (512, 256), (768, 512), (1280, 512), (1792, 512))

    def front(j):
        s = j % 2
        xt = big[s]
        kx, kss, kvar, krstd, kxn, kxnT = ("big", s), ("ss", s), ("var", s), ("rstd", s), ("xn", s), ("xnT", s)
        dma("sp", xt[:], x_d[j * 128:(j + 1) * 128, :], w=[kx])
        S.op("act", lambda e: e.activation(out=junk[:], in_=xt[:], func=AF.Square, accum_out=st(s)),
             r=[kx], w=[kss, "junk"])
        rstd_from_ss(st(s), kss, D, st(4 + s), krstd, st(2 + s), kvar)
        S.op("dve", lambda e: e.tensor_scalar(out=xn_bf[s][:], in0=xt[:], scalar1=st(4 + s), scalar2=None,
                                              op0=ALU.mult), r=[kx, krstd], w=[kxn])
        transposes8(xn_bf[s], kxn, 0)
        S.op("dve", lambda e: e.tensor_tensor(
            out=xnT[s][:], in0=bank_bf(0).rearrange("p (c t) -> p c t", c=8),
            in1=g1.unsqueeze(2).to_broadcast([128, 8, 128]), op=ALU.mult),
            r=[("bank", 0), "gains"], w=[kxnT])

    def back(j):
        s = j % 2
        r8 = j % 8
        kxnT = ("xnT", s)
        qk = qk_tok[s]
        for nt, (c0, ncol) in enumerate(IN_TILES):
            bk = 1 + nt % 2
            def mm(e, c0=c0, ncol=ncol, bk=bk):
                ins = None
                for c in range(8):
                    ins = e.matmul(bank(bk, ncol), lhsT=xnT[s][:, c, :], rhs=w_in_sb[:, c, c0:c0 + ncol],
                                   start=(c == 0), stop=(c == 7))
                return ins
            S.op("pe", mm, r=[kxnT, "w_in"], w=[("bank", bk)])
            kb = ("bank", bk)
            if nt == 0:
                S.op("act", lambda e, bk=bk: e.activation(
                    out=qk[:, 0:512].rearrange("p (i g d) -> p g i d", i=4, g=2, d=64),
                    in_=bank(bk).rearrange("p (g i d) -> p g i d", g=2, i=4, d=64), func=AF.Copy),
                    r=[kb], w=[("qk", s, 0)])
                rope(bank(bk).rearrange("p (h d) -> p h d", h=8), 8, ropeA[s], ("ropeA", s),
                     ropetmp[:, 0:128].rearrange("p (h d) -> p h d", h=8), "ropetmp", kb, j)
                S.op("dve", lambda e: e.tensor_copy(
                    out=qk[:, 0:512].rearrange("p (i g d) -> p g i d", i=4, g=2, d=64)[:, :, :, 0:16],
                    in_=ropeA[s][:].rearrange("p (g i) d -> p g i d", g=2)),
                    r=[("ropeA", s), ("qk", s, 0)], w=[("qk", s, 0)])
            elif nt == 1:
                S.op("dve", lambda e, bk=bk: e.tensor_copy(out=qk[:, 512:640], in_=bank(bk, 128)),
                     r=[kb], w=[("qk", s, 1)])
                rope(bank(bk, 128).rearrange("p (h d) -> p h d", h=2), 2, ropeK[s], ("ropeK", s),
                     ropetmp[:, 128:160].rearrange("p (h d) -> p h d", h=2), "ropetmpk", kb, j)
                S.op("dve", lambda e: e.tensor_copy(
                    out=qk[:, 512:640].rearrange("p (h d) -> p h d", h=2)[:, :, 0:16], in_=ropeK[s][:]),
                    r=[("ropeK", s), ("qk", s, 1)], w=[("qk", s, 1)])
                S.op("act", lambda e, bk=bk: e.activation(
                    out=vA[:, r8, :, 0:64], in_=bank(bk, 128, 128).rearrange("p (g d) -> p g d", g=2), func=AF.Copy),
                    r=[kb], w=[("vA", r8)])
            elif nt == 2:
                S.op("act", lambda e, bk=bk: e.activation(out=qk[:, 640:1152], in_=bank(bk), func=AF.Copy),
                     r=[kb], w=[("qk", s, 2)])
            elif nt == 3:
                S.op("dve", lambda e, bk=bk: e.tensor_copy(out=qk[:, 1152:1664], in_=bank(bk)),
                     r=[kb], w=[("qk", s, 3)])
            else:
                S.op("act", lambda e, bk=bk: e.activation(
                    out=vB[:, r8, :, 0:64], in_=bank(bk).rearrange("p (h d) -> p h d", h=8), func=AF.Copy),
                    r=[kb], w=[("vB", r8)])
            yield

        def tq(e):
            ins = None
            o = bank_bf(0)
            for b in range(4):
                ins = e.transpose(out=o[:, b * 128:(b + 1) * 128], in_=qk[:, b * 128:(b + 1) * 128], identity=ident_bf[:])
            for b in range(4):
                ins = e.transpose(out=o[:, (4 + b) * 128:(5 + b) * 128], in_=qk[:, 640 + b * 128:640 + (b + 1) * 128],
                                  identity=ident_bf[:])
            return ins
        S.op("pe", tq, r=[("qk", s, 0), ("qk", s, 2), "ident"], w=[("bank", 0)])
        S.op("act", lambda e: e.activation(out=qT[:, :, r8, :], in_=bank_bf(0).rearrange("p (b t) -> p b t", b=8),
                                           func=AF.Copy), r=[("bank", 0)], w=[("qT", r8)])
        yield

        def tk(e):
            ins = None
            o = bank_bf(3)
            ins = e.transpose(out=o[:, 0:128], in_=qk[:, 512:640], identity=ident_bf[:])
            for b in range(4):
                ins = e.transpose(out=o[:, (1 + b) * 128:(2 + b) * 128], in_=qk[:, 1152 + b * 128:1152 + (b + 1) * 128],
                                  identity=ident_bf[:])
            return ins
        S.op("pe", tk, r=[("qk", s, 1), ("qk", s, 3), "ident"], w=[("bank", 3)])
        S.op("dve", lambda e: e.tensor_copy(out=kaT[:, r8, :], in_=bank_bf(3)[:, 0:128]),
             r=[("bank", 3)], w=[("kaT", r8)])
        S.op("dve", lambda e: e.tensor_copy(out=kbT[:, :, r8, :],
                                            in_=bank_bf(3)[:, 128:640].rearrange("p (b t) -> p b t", b=4)),
             r=[("bank", 3)], w=[("kbT", r8)])
        yield

    def load_special(idx):
        for h in range(8):
            bs = bstage[0]
            dma("sp", bs[:], biasS_d[idx, :, h * 768:(h + 1) * 768], w=[("bstage", 0)])
            S.op("act", (lambda h, bs: lambda e: e.activation(
                out=expbS[:, h, :, :].rearrange("p j q -> p (j q)"), in_=bs[:], func=AF.Exp))(h, bs),
                r=[("bstage", 0)], w=[("expbS", h)])

    def attention(e_, filler=None, after_a=None):
        j = e_ + 2
        rq = j % 8
        so = 2 + e_ % 2
        o_tok = big[so]
        ko = ("big", so)
        for g in range(2):
            for di, dj in enumerate((-1, 0, 1)):
                rk = (j + dj) % 8
                bk = 3 + di % 2
                S.op("pe", lambda e, g=g, rk=rk, bk=bk: e.matmul(
                    bank(bk), lhsT=kaT[64 * g:64 * g + 64, rk, :], rhs=qT[64 * g:64 * g + 64, 0:4, rq, :],
                    start=True, stop=True), r=[("kaT", rk), ("qT", rq)], w=[("bank", bk)])
                S.op("act", lambda e, g=g, di=di, bk=bk: e.activation(
                    out=pA[g][:, di, :], in_=bank(bk), func=AF.Exp, scale=0.125),
                    r=[("bank", bk)], w=[("pA", g, di)])
                if dj != 0:
                    mi = (2 if e_ == 1 else 0) if dj < 0 else (3 if e_ == NO else 1)
                    S.op("dve", lambda e, g=g, di=di, mi=mi: e.tensor_tensor(
                        out=pA[g][:, di, :].rearrange("p (i q) -> p i q", i=4),
                        in0=pA[g][:, di, :].rearrange("p (i q) -> p i q", i=4),
                        in1=maskA_bf[:, mi:mi + 1, :].to_broadcast([128, 4, 128]), op=ALU.mult),
                        r=[("pA", g, di), "maskA"], w=[("pA", g, di)])

            def pv(e, g=g):
                ins = None
                for i in range(4):
                    for di, dj in enumerate((-1, 0, 1)):
                        rk = (j + dj) % 8
                        ins = e.matmul(bank(5 + g, 65, i * 65), lhsT=pA[g][:, di, i * 128:(i + 1) * 128],
                                       rhs=vA[:, rk, g, :], start=(di == 0), stop=(di == 2))
                return ins
            S.op("pe", pv, r=[("pA", g, 0), ("pA", g, 1), ("pA", g, 2)] + [("vA", (j + dj) % 8) for dj in (-1, 0, 1)],
                 w=[("bank", 5 + g)])
            S.op("dve", lambda e, g=g: e.tensor_tensor(
                out=den[:, 4 * g:4 * g + 4], in0=bank(5 + g, 260).rearrange("p (i d) -> p i d", i=4)[:, :, 64],
                in1=esink[:, 4 * g:4 * g + 4], op=ALU.add), r=[("bank", 5 + g), "esink"], w=[("denA", g)])
            S.op("dve", lambda e, g=g: e.reciprocal(out=den[:, 8 + 4 * g:12 + 4 * g], in_=den[:, 4 * g:4 * g + 4]),
                 r=[("denA", g)], w=[("rdenA", g)])
            S.op("dve", lambda e, g=g: e.tensor_tensor(
                out=o_tok[:, 256 * g:256 * g + 256].rearrange("p (i d) -> p i d", i=4),
                in0=bank(5 + g, 260).rearrange("p (i d) -> p i d", i=4)[:, :, 0:64],
                in1=den[:, 8 + 4 * g:12 + 4 * g].unsqueeze(2).to_broadcast([128, 4, 64]), op=ALU.mult),
                r=[("bank", 5 + g), ("rdenA", g)], w=[ko])
        if after_a is not None:
            after_a()
        if e_ in SPECIAL_E:
            djs = DJ_TOP if e_ in (1, 2) else DJ_BOT
            tab = expbS
            tkey = "expbS"
        else:
            djs = DJ_GEN
            tab = expbG
            tkey = "expbG"
        nd = len(djs)

        def sT(h):
            b, hp = h // 2, h % 2
            def fn(e):
                ins = None
                for di, dj in enumerate(djs):
                    rk = (j + dj) % 8
                    ins = e.matmul(psum[:, 3 * 512 + di * 128:3 * 512 + (di + 1) * 128],
                                   lhsT=kbT[64 * hp:64 * hp + 64, b, rk, :], rhs=qT[64 * hp:64 * hp + 64, 4 + b, rq, :],
                                   start=True, stop=True)
                return ins
            S.op("pe", fn, r=[("kbT", (j + dj) % 8) for dj in djs] + [("qT", rq)], w=[("bank", 3), ("bank", 4)])
            S.op("act", lambda e: e.activation(out=pB[hp][:, 0:nd * 128], in_=psum[:, 3 * 512:3 * 512 + nd * 128],
                                               func=AF.Exp, scale=0.125),
                 r=[("bank", 3), ("bank", 4)], w=[("pB", hp)])
            S.op("dve", lambda e: e.tensor_tensor(
                out=pB[hp][:, 0:nd * 128], in0=pB[hp][:, 0:nd * 128],
                in1=tab[:, h, 0:nd, :].rearrange("p j q -> p (j q)"), op=ALU.mult),
                r=[("pB", hp), (tkey, h)], w=[("pB", hp)])

        def pvB(h):
            hp = h % 2
            def fn(e):
                ins = None
                for di, dj in enumerate(djs):
                    rk = (j + dj) % 8
                    ins = e.matmul(bank(5 + h // 4, 65, (h % 4) * 65), lhsT=pB[hp][:, di * 128:(di + 1) * 128],
                                   rhs=vB[:, rk, h, :], start=(di == 0), stop=(di == nd - 1))
                return ins
            S.op("pe", fn, r=[("pB", hp)] + [("vB", (j + dj) % 8) for dj in djs], w=[("bank", 5 + h // 4)])

        for h in range(9):
            if h < 8:
                sT(h)
                if filler is not None:
                    next(filler, None)
            if h >= 1:
                pvB(h - 1)
        for g in range(2):
            S.op("dve", lambda e, g=g: e.reciprocal(
                out=den[:, 16 + 4 * g:20 + 4 * g], in_=bank(5 + g, 260).rearrange("p (i d) -> p i d", i=4)[:, :, 64]),
                r=[("bank", 5 + g)], w=[("rdenB", g)])
            S.op("dve", lambda e, g=g: e.tensor_tensor(
                out=o_tok[:, 512 + 256 * g:768 + 256 * g].rearrange("p (i d) -> p i d", i=4),
                in0=bank(5 + g, 260).rearrange("p (i d) -> p i d", i=4)[:, :, 0:64],
                in1=den[:, 16 + 4 * g:20 + 4 * g].unsqueeze(2).to_broadcast([128, 4, 64]), op=ALU.mult),
                r=[("bank", 5 + g), ("rdenB", g)], w=[ko])
        for half in range(2):
            S.op("act", lambda e, half=half: e.activation(
                out=junk[:, 0:512], in_=o_tok[:, 512 * half:512 * half + 512], func=AF.Square,
                accum_out=st(8 + half)), r=[ko], w=[("ssm", half), "junk"])
            rstd_from_ss(st(8 + half), ("ssm", half), 512, st(12 + half), ("rstdm", half), st(10 + half), ("varm", half))
            S.op("dve", lambda e, half=half: e.tensor_scalar(
                out=mixn[:, 512 * half:512 * half + 512],
                in0=o_tok[:, 512 * half:512 * half + 512], scalar1=st(12 + half), scalar2=None, op0=ALU.mult),
                r=[ko, ("rstdm", half)], w=["mixn"])

    def outproj(e_):
        j = e_ + 2
        transposes8(mixn, "mixn", 0)
        S.op("dve", lambda e: e.tensor_tensor(
            out=mixT[:], in0=bank_bf(0).rearrange("p (c t) -> p c t", c=8),
            in1=gab.unsqueeze(2).to_broadcast([128, 8, 128]), op=ALU.mult),
            r=[("bank", 0), "gains"], w=["mixT"])
        h1t = big[4]
        kh = ("big", 4)
        dma("sp", h1t[:], x_d[j * 128:(j + 1) * 128, :], w=[kh])
        for n in range(2):
            bk = 1 + n
            def mm(e, n=n, bk=bk):
                ins = None
                for c in range(8):
                    ins = e.matmul(bank(bk), lhsT=mixT[:, c, :], rhs=w_out_sb[:, c, n * 512:(n + 1) * 512],
                                   start=(c == 0), stop=(c == 7))
                return ins
            S.op("pe", mm, r=["mixT", "w_out"], w=[("bank", bk)])
            S.op("dve", lambda e, n=n, bk=bk: e.tensor_tensor(
                out=h1t[:, n * 512:(n + 1) * 512], in0=bank(bk), in1=h1t[:, n * 512:(n + 1) * 512], op=ALU.add),
                r=[("bank", bk), kh], w=[kh])
        dma("sp", h1d[e_ * 128:(e_ + 1) * 128, :], h1t[:], r=[kh], w=[("h1d", e_)])

    mixn = sb("mixn", [128, 1024], BF16)

    LOOK = 6
    spec_idx = {1: 0, 2: 1, NO - 1: 2, NO: 3}
    front(0)
    pending_tail = [None]
    for t in range(NX + LOOK):
        if t + 1 < NX:
            front(t + 1)
        filler = back(t) if t < NX else None
        e_ = t - LOOK
        if 0 <= e_ < NE:
            if e_ in spec_idx:
                load_special(spec_idx[e_])
            prev = pending_tail[0]
            attention(e_, filler=filler,
                      after_a=(lambda prev=prev: outproj(prev)) if prev is not None else None)
            pending_tail[0] = e_
        if filler is not None:
            for _ in filler:
                pass
        if t == NX - 1:
            for c in range(6):
                dma("pool", wg_sb[:, c, :], w_gate_d[c * 128:(c + 1) * 128, :], w=["wgA", "w_in"])
    outproj(pending_tail[0])

    bar = S.barrier_events()
    FG = ((0, 768), (768, 1536), (1536, 2304), (2304, DFF))
    first = True
    for gi, (f0, f1) in enumerate(FG):
        for c in range(6, 8):
            dma("pool", wg_sb[:, c, f0:f1], w_gate_d[c * 128:(c + 1) * 128, f0:f1], w=[("wg", gi)],
                extra=bar if first else ())
            first = False
        for c in range(8):
            dma("pool", wu_sb[:, c, f0:f1], w_up_d[c * 128:(c + 1) * 128, f0:f1], w=[("wu", gi)])
        for fc in range(f0 // 128, f1 // 128):
            dma("pool", wd_sb[:, fc, :], w_down_d[fc * 128:(fc + 1) * 128, :], w=[("wd", fc)])

    def prep2(e_):
        rh = e_ % 5
        h1r = big[rh]
        kh = ("big", rh)
        sl = (e_ - 1) % 8
        dma("sp", h1r[:], h1d[e_ * 128:(e_ + 1) * 128, :], r=[("h1d", e_)], w=[kh])
        S.op("act", lambda e: e.activation(out=junk[:], in_=h1r[:], func=AF.Square, accum_out=st(16)),
             r=[kh], w=["ss2", "junk"])
        rstd_from_ss(st(16), "ss2", D, st(18), "rstd2", st(17), "var2")
        if e_ in (0, NE - 1):
            hv = hvalid[:, 0:1] if e_ == 0 else hvalid[:, 1:2]
            S.op("dve", lambda e: e.tensor_tensor(out=st(18), in0=st(18), in1=hv, op=ALU.mult),
                 r=["rstd2", "hvalid"], w=["rstd2"])
        S.op("dve", lambda e: e.tensor_scalar(out=mixn[:], in0=h1r[:], scalar1=st(18), scalar2=None, op0=ALU.mult),
             r=[kh, "rstd2"], w=["mixn"])
        transposes8(mixn, "mixn", 0)
        S.op("dve", lambda e: e.tensor_tensor(
            out=hn2T[:, :, sl, :], in0=bank_bf(0).rearrange("p (c t) -> p c t", c=8),
            in1=g2.unsqueeze(2).to_broadcast([128, 8, 128]), op=ALU.mult),
            r=[("bank", 0), "gains"], w=[("hn2T", sl)])

    def ffn(s_):
        ea = 2 * s_ + 1
        sa = (2 * s_) % 8
        sl_l = (2 * s_ - 1) % 8
        sl_r = (2 * s_ + 2) % 8
        rkeys = [("hn2T", sa), ("hn2T", sa + 1), ("hn2T", sl_l), ("hn2T", sl_r)]

        wi = win[0]
        kw = ("win", 0)
        S.op("pool", lambda e: e.tensor_copy(out=wi[:, :, 1:257].rearrange("p c (a t) -> p c a t", a=2),
                                             in_=hn2T[:, :, sa:sa + 2, :]),
             r=[("hn2T", sa), ("hn2T", sa + 1)], w=[kw])
        S.op("pool", lambda e: e.tensor_copy(out=wi[:, :, 0:1], in_=hn2T[:, :, sl_l, 127:128]),
             r=[("hn2T", sl_l), kw], w=[kw])
        S.op("pool", lambda e: e.tensor_copy(out=wi[:, :, 257:258], in_=hn2T[:, :, sl_r, 0:1]),
             r=[("hn2T", sl_r), kw], w=[kw])

        def gu(fc):
            pb = fc % 2
            bk = 1 + pb
            ub = 3 if pb == 0 else 0
            def fn(e):
                ins = None
                fs = slice(fc * 128, (fc + 1) * 128)
                for c in range(8):
                    ins = e.matmul(bank(bk, 258), lhsT=wg_sb[:, c, fs], rhs=wi[:, c, 0:258],
                                   start=(c == 0), stop=(c == 7))
                for c in range(8):
                    ins = e.matmul(bank(ub, 256), lhsT=wu_sb[:, c, fs], rhs=wi[:, c, 1:257],
                                   start=(c == 0), stop=(c == 7))
                return ins
            S.op("pe", fn, r=[kw, "wgA", ("wg", fc // 6), ("wu", fc // 6)], w=[("bank", bk), ("bank", ub)])

        def chain(fc):
            pb = fc % 2
            bk = 1 + pb
            ub = 3 if pb == 0 else 0
            acc = accs[pb]
            sg = sgs[pb]
            kb = ("bank", bk)
            gps = bank(bk, 258)
            S.op("act", lambda e: e.activation(out=usb[pb][:], in_=bank(ub, 256), func=AF.Copy),
                 r=[("bank", ub)], w=[("usb", pb)])
            S.op("act", lambda e: e.activation(out=acc[:], in_=gps[:, 1:257], func=AF.Identity,
                                               bias=cb[:, fc:fc + 1], scale=cw[:, fc, 1:2]),
                 r=[kb, "cw", "cb"], w=[("acc", pb)])
            S.op("dve", lambda e: e.scalar_tensor_tensor(out=acc[:], in0=gps[:, 0:256], scalar=cw[:, fc, 0:1],
                                                         in1=acc[:], op0=ALU.mult, op1=ALU.add),
                 r=[kb, ("acc", pb)], w=[("acc", pb)])
            S.op("dve", lambda e: e.scalar_tensor_tensor(out=acc[:], in0=gps[:, 2:258], scalar=cw[:, fc, 2:3],
                                                         in1=acc[:], op0=ALU.mult, op1=ALU.add),
                 r=[kb, ("acc", pb)], w=[("acc", pb)])
            S.op("act", lambda e: e.activation(out=sg[:], in_=acc[:], func=AF.Silu),
                 r=[("acc", pb)], w=[("sg", pb)])
            S.op("pool", lambda e: e.tensor_tensor(out=actc[pb][:], in0=sg[:], in1=usb[pb][:], op=ALU.mult),
                 r=[("sg", pb), ("usb", pb)], w=[("actc", pb)])

        def down(fc):
            pb = fc % 2
            def fn(e):
                ins = None
                for tt in range(2):
                    for n in range(2):
                        ins = e.matmul(bank(4 + tt * 2 + n), lhsT=actc[pb][:, tt * 128:(tt + 1) * 128],
                                       rhs=wd_sb[:, fc, n * 512:(n + 1) * 512], start=(fc == 0), stop=(fc == NFC - 1))
                return ins
            S.op("pe", fn, r=[("actc", pb), ("wd", fc)], w=[("bank", 4), ("bank", 5), ("bank", 6), ("bank", 7)])

        gu(0)
        gu(1)
        for fc in range(NFC):
            chain(fc)
            if fc + 2 < NFC:
                gu(fc + 2)
            down(fc)
        for tt in range(2):
            e_ = ea + tt
            rh = e_ % 5
            h1r = big[rh]
            kh = ("big", rh)
            for n in range(2):
                S.op("dve", lambda e, tt=tt, n=n, h1r=h1r: e.tensor_tensor(
                    out=h1r[:, n * 512:(n + 1) * 512], in0=bank(4 + tt * 2 + n), in1=h1r[:, n * 512:(n + 1) * 512],
                    op=ALU.add), r=[("bank", 4 + tt * 2 + n), kh], w=[kh])
            S.op("act", lambda e, h1r=h1r, tt=tt: e.activation(out=junk[:], in_=h1r[:], func=AF.Square,
                                                               accum_out=st(20 + tt)), r=[kh], w=[("ss3", tt), "junk"])
            rstd_from_ss(st(20 + tt), ("ss3", tt), D, st(24 + tt), ("rstd3", tt), st(22 + tt), ("var3", tt))
            S.op("dve", lambda e, h1r=h1r, tt=tt: e.scalar_tensor_tensor(
                out=h1r[:], in0=h1r[:], scalar=st(24 + tt), in1=gfin[:], op0=ALU.mult, op1=ALU.mult),
                r=[kh, ("rstd3", tt), "gfin"], w=[kh])
            dma("sp", out_d[(e_ - 1) * 128:e_ * 128, :], h1r[:], r=[kh], w=[("out", e_)], is_out=True)

    for e_ in range(NE):
        prep2(e_)
        if e_ >= 3 and (e_ - 3) % 2 == 0:
            ffn((e_ - 3) // 2)

    def semh(sk):
        return sem_dma[sk[1]] if isinstance(sk, tuple) else sem_eng[sk]

    def emit(eng_name, e, final_waits=()):
        for waits, fn, ev in S.ops[eng_name]:
            for sk, v in waits:
                e.wait_ge(semh(sk), v)
            ins = fn(e)
            if isinstance(ev[0], tuple):
                ins.then_inc(semh(ev[0]), 16)
            else:
                ins.then_inc(semh(ev[0]), 1)
        for sk, v in final_waits:
            e.wait_ge(semh(sk), v)

    fin = {}
    for ev in S.out_events:
        fin[ev[0]] = max(fin.get(ev[0], 0), ev[1])

    with nc.Block() as block:
        @block.sync
        def _(e):
            emit("sp", e, final_waits=list(fin.items()))

        @block.tensor
        def _(e):
            emit("pe", e)

        @block.scalar
        def _(e):
            emit("act", e)

        @block.vector
        def _(e):
            emit("dve", e)

        @block.gpsimd
        def _(e):
            emit("pool", e)
    es.close()
    return nc


def _nbr_table(rpb, m, djs):
    rows = SEQ // 64
    nd = len(djs)
    tab = np.full((128, 8, nd, 128), MASKV, dtype=np.float32)
    qr = np.repeat(np.arange(2), 64) + 2 * m
    qc = np.tile(np.arange(64), 2)
    rs = np.clip(qr - 4, 0, rows - 8)
    cs = np.clip(qc - 8, 0, 64 - 16)
    for di, dj in enumerate(djs):
        mk = m + dj
        if mk < 0 or mk >= rows // 2:
            continue
        kr = np.repeat(np.arange(2), 64) + 2 * mk
        kc = np.tile(np.arange(64), 2)
        valid = ((kr[:, None] >= rs[None, :]) & (kr[:, None] < rs[None, :] + 8) &
                 (kc[:, None] >= cs[None, :]) & (kc[:, None] < cs[None, :] + 16))
        ro = np.clip(kr[:, None] - qr[None, :] + 7, 0, 14)
        co = np.clip(kc[:, None] - qc[None, :] + 15, 0, 30)
        for h in range(8):
            g = rpb[h][ro, co]
            tab[:, h, di, :] = np.where(valid, g, np.float32(MASKV))
    return tab


_NC_CACHE = {}


def _shared_inputs(norm_mix, w_in, sink, rpb, norm_a, norm_b, w_out, norm_ffn,
                   w_gate, w_up, conv_w, conv_b, w_down, norm_final):
    f32 = np.float32

    def fm(v, nchunk):
        return np.ascontiguousarray(np.asarray(v, f32).reshape(nchunk, 128).T)

    gains = np.concatenate([fm(np.asarray(norm_mix)[0], 8), fm(np.asarray(norm_ffn)[0], 8),
                            fm(np.concatenate([np.asarray(norm_a)[0], np.asarray(norm_b)[0]]), 8)], axis=1)
    cwv = np.asarray(conv_w, f32)[0]
    rpb0 = np.asarray(rpb, f32)[0]
    return {
        "w_in": np.ascontiguousarray(np.asarray(w_in, f32)[0]),
        "w_out": np.ascontiguousarray(np.asarray(w_out, f32)[0]),
        "w_gate": np.ascontiguousarray(np.asarray(w_gate, f32)[0]),
        "w_up": np.ascontiguousarray(np.asarray(w_up, f32)[0]),
        "w_down": np.ascontiguousarray(np.asarray(w_down, f32)[0]),
        "ident": np.eye(128, dtype=f32),
        "gains": np.ascontiguousarray(gains, dtype=f32),
        "gfin": np.ascontiguousarray(np.asarray(norm_final, f32)),
        "biasG": np.ascontiguousarray(_nbr_table(rpb0, 10, DJ_GEN).reshape(128, 8 * 5 * 128)),
        "sinkb": np.ascontiguousarray(np.broadcast_to(np.asarray(sink, f32)[0][None, :], (128, 8))),
        "cw": np.ascontiguousarray(np.stack([fm(cwv[k], NFC) for k in range(3)], axis=2).reshape(128, NFC * 3)),
        "cb": fm(np.asarray(conv_b, f32)[0], NFC),
    }, rpb0


def _core_inputs(xb, own_start, NO, rpb0):
    f32 = np.float32
    NX = NO + 6
    t0 = own_start - 384
    xe = np.zeros((NX * 128, D), f32)
    lo, hi = max(t0, 0), min(t0 + NX * 128, SEQ)
    xe[lo - t0:hi - t0] = xb[lo:hi]
    inv_freq = (np.float32(500000.0) ** (-np.arange(0, 16, 2, dtype=f32) / np.float32(16))).astype(f32)
    pos = (t0 + np.arange(NX * 128)).astype(f32)
    ang = (pos[:, None] * inv_freq[None, :]).astype(f32)
    cosv, sinv = np.cos(ang).astype(f32), np.sin(ang).astype(f32)
    cs2 = np.concatenate([cosv, cosv], axis=1).reshape(NX, 128, 16).transpose(1, 0, 2)
    sn2 = np.concatenate([-sinv, sinv], axis=1).reshape(NX, 128, 16).transpose(1, 0, 2)
    kk = np.arange(128)[:, None]
    qq = np.arange(128)[None, :]
    m_prev = (kk >= qq).astype(f32)
    m_next = (kk <= qq).astype(f32)
    zeros = np.zeros((128, 128), f32)
    at_start = own_start == 0
    at_end = own_start + NO * 128 == SEQ
    maskA = np.stack([m_prev, m_next, zeros if at_start else m_prev, zeros if at_end else m_next], axis=1)
    mbase = own_start // 128
    spec = []
    for e_, djs in ((1, DJ_TOP), (2, DJ_TOP), (NO - 1, DJ_BOT), (NO, DJ_BOT)):
        spec.append(_nbr_table(rpb0, mbase + e_ - 1, djs).reshape(128, 8 * 6 * 128))
    hvalid = np.empty((128, 2), f32)
    hvalid[:, 0] = 0.0 if at_start else 1.0
    hvalid[:, 1] = 0.0 if at_end else 1.0
    return {
        "x": xe,
        "cs2": np.ascontiguousarray(cs2.reshape(128, NX * 16)),
        "sn2": np.ascontiguousarray(sn2.reshape(128, NX * 16)),
        "maskA": np.ascontiguousarray(maskA.reshape(128, 4 * 128)),
        "biasS": np.ascontiguousarray(np.stack(spec, axis=0)),
        "hvalid": hvalid,
    }


def kernel(x, norm_mix, w_in, sink, rpb, norm_a, norm_b, w_out, norm_ffn,
           w_gate, w_up, conv_w, conv_b, w_down, norm_final):
    f32 = np.float32
    x = np.asarray(x, f32)
    shared, rpb0 = _shared_inputs(norm_mix, w_in, sink, rpb, norm_a, norm_b, w_out, norm_ffn,
                                  w_gate, w_up, conv_w, conv_b, w_down, norm_final)
    NO = NO_FULL
    OWN = NO * 128
    in_maps = []
    for c in range(NCORES):
        b, half = c // 2, c % 2
        m = dict(shared)
        m.update(_core_inputs(x[b], half * OWN, NO, rpb0))
        in_maps.append(m)
    if "nc" not in _NC_CACHE:
        _NC_CACHE["nc"] = build_program(NO)
    res = run_bass_kernel_spmd(_NC_CACHE["nc"], in_maps, core_ids=list(range(NCORES)))
    out = np.empty((4, SEQ, D), f32)
    for c in range(NCORES):
        b, half = c // 2, c % 2
        out[b, half * OWN:(half + 1) * OWN] = res.results[c]["out"]
    return out
```

```python
import math
from contextlib import ExitStack
import numpy as np
import concourse.bass as bass
import concourse.mybir as mybir
from concourse.bass_utils import run_bass_kernel_spmd

F32 = mybir.dt.float32
BF16 = mybir.dt.bfloat16
ALU = mybir.AluOpType
AF = mybir.ActivationFunctionType

D = 1024
SEQ = 8192
NCORES = 8
NO_FULL = 32
DFF = 2816
NFC = 22
EPS = 1e-6
MASKV = -100.0
DJ_GEN = (-2, -1, 0, 1, 2)
DJ_TOP = (-2, -1, 0, 1, 2, 3)
DJ_BOT = (-3, -2, -1, 0, 1, 2)
NDSEM = 48
NDSEM_SP = 16


class Sched:
    ENG = ("pe", "act", "dve", "pool", "sp")

    def __init__(self):
        self.ops = {n: [] for n in self.ENG}
        self.cnt = {n: 0 for n in self.ENG}
        self.waited = {n: {} for n in self.ENG}
        self.bufs = {}
        self.dcnt = [0] * NDSEM
        self.dnext = {"sp": 0, "pool": 0}
        self.out_events = []

    def op(self, eng, fn, r=(), w=(), dma=False, extra=(), is_out=False):
        deps = {}

        def need(ev):
            if ev is not None:
                if deps.get(ev[0], 0) < ev[1]:
                    deps[ev[0]] = ev[1]

        for k in r:
            b = self.bufs.get(k)
            if b is not None:
                need(b[0])
                if isinstance(k, tuple) and k[0] == "bank":
                    for ev in b[1]:
                        if ev[0] != eng:
                            need(ev)
        for k in w:
            b = self.bufs.get(k)
            if b is not None:
                need(b[0])
                for ev in b[1]:
                    need(ev)
        for ev in extra:
            need(ev)
        if dma:
            lo_, n_ = (0, NDSEM_SP) if eng == "sp" else (NDSEM_SP, NDSEM - NDSEM_SP)
            k = lo_ + self.dnext[eng]
            self.dnext[eng] = (self.dnext[eng] + 1) % n_
            if self.dcnt[k] > 0:
                need((("d", k), self.dcnt[k]))
            self.dcnt[k] += 16
            ev = (("d", k), self.dcnt[k])
        else:
            self.cnt[eng] += 1
            ev = (eng, self.cnt[eng])
        waits = []
        for sk, v in deps.items():
            if sk == eng and eng == "pe":
                continue
            if self.waited[eng].get(sk, 0) >= v:
                continue
            self.waited[eng][sk] = v
            waits.append((sk, v))
        self.ops[eng].append((waits, fn, ev))
        for k in r:
            self.bufs.setdefault(k, [None, []])[1].append(ev)
        for k in w:
            self.bufs[k] = [ev, []]
        if is_out:
            self.out_events.append(ev)
        return ev

    def barrier_events(self):
        evs = [(n, self.cnt[n]) for n in self.ENG if self.cnt[n] > 0]
        evs += [(("d", k), self.dcnt[k]) for k in range(NDSEM) if self.dcnt[k] > 0]
        return evs


def build_program(NO=NO_FULL):
    NX, NE, OWN = NO + 6, NO + 2, NO * 128
    SPECIAL_E = (1, 2, NO - 1, NO)
    nc = bass.Bass("TRN2", target_bir_lowering=False)
    S = Sched()

    def din(name, shape):
        return nc.dram_tensor(name, list(shape), F32, kind="ExternalInput").ap()

    x_d = din("x", [NX * 128, D])
    w_in_d = din("w_in", [D, 2304])
    w_out_d = din("w_out", [D, D])
    w_gate_d = din("w_gate", [D, DFF])
    w_up_d = din("w_up", [D, DFF])
    w_down_d = din("w_down", [DFF, D])
    ident_d = din("ident", [128, 128])
    gains_d = din("gains", [128, 24])
    gfin_d = din("gfin", [D])
    cs2_d = din("cs2", [128, NX * 16])
    sn2_d = din("sn2", [128, NX * 16])
    maskA_d = din("maskA", [128, 4 * 128])
    biasG_d = din("biasG", [128, 8 * 5 * 128])
    biasS_d = din("biasS", [4, 128, 8 * 6 * 128])
    sinkb_d = din("sinkb", [128, 8])
    cw_d = din("cw", [128, NFC * 3])
    cb_d = din("cb", [128, NFC])
    hvalid_d = din("hvalid", [128, 2])
    out_d = nc.dram_tensor("out", [OWN, D], F32, kind="ExternalOutput").ap()
    h1d = nc.dram_tensor("h1scratch", [NE * 128, D], F32, kind="Internal").ap()

    es = ExitStack()

    def sb(name, shape, dt):
        return es.enter_context(nc.sbuf_tensor(name, list(shape), dt))

    ARENA = 67584
    arena = sb("arena", [128, ARENA], BF16)
    aoff = [0]

    def carve(n):
        o = aoff[0]
        aoff[0] += n
        assert aoff[0] <= ARENA, aoff[0]
        return arena[:, o:o + n]

    w_in_sb = carve(8 * 2304).rearrange("p (c e) -> p c e", c=8)
    w_out_sb = carve(8 * 1024).rearrange("p (c e) -> p c e", c=8)
    qT = carve(8 * 8 * 128).rearrange("p (b s t) -> p b s t", b=8, s=8)
    kaT = carve(8 * 128).rearrange("p (s t) -> p s t", s=8)
    kbT = carve(4 * 8 * 128).rearrange("p (b s t) -> p b s t", b=4, s=8)
    vA = carve(8 * 2 * 65).rearrange("p (s g d) -> p s g d", s=8, g=2)
    vB = carve(8 * 8 * 65).rearrange("p (s h d) -> p s h d", s=8, h=8)
    expbG = carve(8 * 5 * 128).rearrange("p (h j q) -> p h j q", h=8, j=5)
    expbS = carve(8 * 6 * 128).rearrange("p (h j q) -> p h j q", h=8, j=6)
    pA = [carve(3 * 512).rearrange("p (j q) -> p j q", j=3) for _ in range(2)]
    xn_bf = [carve(1024) for _ in range(2)]
    xnT = [carve(1024).rearrange("p (c t) -> p c t", c=8) for _ in range(2)]
    qk_tok = [carve(1664) for _ in range(2)]
    p1_end = aoff[0]
    aoff[0] = 0
    wg_sb = carve(8 * DFF).rearrange("p (c f) -> p c f", c=8)
    wu_sb = carve(8 * DFF).rearrange("p (c f) -> p c f", c=8)
    wd_sb = carve(NFC * 1024).rearrange("p (c e) -> p c e", c=NFC)

    ident_bf = sb("ident_bf", [128, 128], BF16)
    maskA_bf = sb("maskA_bf", [128, 4, 128], BF16)
    junk = sb("junk", [128, 1024], BF16)
    pB = [sb(f"pB{i}", [128, 768], BF16) for i in range(2)]
    mixT = sb("mixT", [128, 8, 128], BF16)
    hn2T = sb("hn2T", [128, 8, 8, 128], BF16)
    actc = [sb(f"actc{i}", [128, 256], BF16) for i in range(2)]

    big = [sb(f"big{i}", [128, 1024], F32) for i in range(5)]
    ropetmp = sb("ropetmp", [128, 160], F32)
    bstage = [sb("bstage0", [128, 768], F32)]
    gains = sb("gains_sb", [128, 24], F32)
    gfin = sb("gfin_sb", [128, 1024], F32)
    cs2 = sb("cs2_sb", [128, NX, 16], F32)
    sn2 = sb("sn2_sb", [128, NX, 16], F32)
    esink = sb("esink", [128, 8], F32)
    cw = sb("cw_sb", [128, NFC, 3], F32)
    cb = sb("cb_sb", [128, NFC], F32)
    hvalid = sb("hvalid_sb", [128, 2], F32)
    cm05 = sb("cm05", [128, 1], F32)
    stat = sb("stat", [128, 64], F32)
    ropeA = [sb(f"ropeA{i}", [128, 8, 16], F32) for i in range(2)]
    ropeK = [sb(f"ropeK{i}", [128, 2, 16], F32) for i in range(2)]
    den = sb("den", [128, 32], F32)
    accs = [sb(f"acc{i}", [128, 256], F32) for i in range(2)]
    sgs = [sb(f"sg{i}", [128, 256], F32) for i in range(2)]
    usb = [sb(f"usb{i}", [128, 256], F32) for i in range(2)]
    win = [sb("win0", [128, 8, 258], BF16)]

    psum = es.enter_context(nc.psum_tensor("psum", [128, 4096], F32))

    def bank(i, n=512, off=0):
        return psum[:, i * 512 + off:i * 512 + off + n]

    def bank_bf(i):
        return psum[:, i * 512:(i + 1) * 512].bitcast(BF16)

    sem_eng = {n: es.enter_context(nc.semaphore(f"s_{n}")) for n in Sched.ENG}
    sem_dma = [es.enter_context(nc.semaphore(f"s_d{k}")) for k in range(NDSEM)]

    g1 = gains[:, 0:8]
    g2 = gains[:, 8:16]
    gab = gains[:, 16:24]

    def st(i):
        return stat[:, i:i + 1]

    def dma(eng, out, in_, r=(), w=(), extra=(), is_out=False):
        return S.op(eng, lambda e: e.dma_start(out=out, in_=in_), r=r, w=w, dma=True,
                    extra=extra, is_out=is_out)

    def rstd_from_ss(ss_ap, ss_key, n, out_ap, out_key, tmp_ap, tmp_key):
        S.op("dve", lambda e: e.tensor_scalar(out=tmp_ap, in0=ss_ap, scalar1=1.0 / n, scalar2=EPS,
                                              op0=ALU.mult, op1=ALU.add), r=[ss_key], w=[tmp_key])
        S.op("pool", lambda e: e.tensor_tensor(out=out_ap, in0=tmp_ap, in1=cm05[:, 0:1], op=ALU.pow),
             r=[tmp_key, "cm05"], w=[out_key])

    def transposes8(src_bf, src_key, dst_bank):
        def fn(e):
            ins = None
            o = bank_bf(dst_bank)
            for c in range(8):
                ins = e.transpose(out=o[:, c * 128:(c + 1) * 128], in_=src_bf[:, c * 128:(c + 1) * 128],
                                  identity=ident_bf[:])
            return ins
        S.op("pe", fn, r=[src_key, "ident"], w=[("bank", dst_bank)])

    S.op("pool", lambda e: e.memset(cm05[:], -0.5), w=["cm05"])
    S.op("pool", lambda e: e.memset(vA.rearrange("p s g d -> p (s g d)"), 1.0),
         w=[("vA", s) for s in range(8)])
    S.op("pool", lambda e: e.memset(vB.rearrange("p s h d -> p (s h d)"), 1.0),
         w=[("vB", s) for s in range(8)])
    dma("pool", ident_bf[:], ident_d, w=["ident"])
    dma("pool", maskA_bf.rearrange("p a q -> p (a q)"), maskA_d, w=["maskA"])
    dma("sp", gains[:], gains_d, w=["gains"])
    dma("sp", cs2.rearrange("p j d -> p (j d)"), cs2_d, w=["cs2"])
    dma("sp", sn2.rearrange("p j d -> p (j d)"), sn2_d, w=["sn2"])
    dma("sp", esink[:], sinkb_d, w=["esink_raw"])
    S.op("act", lambda e: e.activation(out=esink[:], in_=esink[:], func=AF.Exp), r=["esink_raw"], w=["esink"])
    dma("sp", cw.rearrange("p c k -> p (c k)"), cw_d, w=["cw"])
    dma("sp", cb[:], cb_d, w=["cb"])
    dma("sp", hvalid[:], hvalid_d, w=["hvalid"])
    dma("sp", gfin[:], gfin_d.partition_broadcast(128), w=["gfin"])
    for c in range(8):
        dma("pool", w_in_sb[:, c, :], w_in_d[c * 128:(c + 1) * 128, :], w=["w_in"])
    for c in range(8):
        dma("pool", w_out_sb[:, c, :], w_out_d[c * 128:(c + 1) * 128, :], w=["w_out"])
    for h in range(8):
        bs = bstage[0]
        dma("sp", bs[:, 0:640], biasG_d[:, h * 640:(h + 1) * 640], w=[("bstage", 0)])
        S.op("act", (lambda h, bs: lambda e: e.activation(
            out=expbG[:, h, :, :].rearrange("p j q -> p (j q)"), in_=bs[:, 0:640], func=AF.Exp))(h, bs),
            r=[("bstage", 0)], w=[("expbG", h)])

    def rope(xv, nh, rp, krp, tp, ktp, kb, j):
        S.op("dve", lambda e: e.tensor_tensor(out=rp[:], in0=xv[:, :, 0:16],
                                              in1=cs2[:, j:j + 1, :].to_broadcast([128, nh, 16]), op=ALU.mult),
             r=[kb, "cs2"], w=[krp])
        S.op("dve", lambda e: e.tensor_tensor(out=tp[:, :, 0:8], in0=xv[:, :, 8:16],
                                              in1=sn2[:, j:j + 1, 0:8].to_broadcast([128, nh, 8]), op=ALU.mult),
             r=[kb, "sn2"], w=[ktp])
        S.op("dve", lambda e: e.tensor_tensor(out=tp[:, :, 8:16], in0=xv[:, :, 0:8],
                                              in1=sn2[:, j:j + 1, 8:16].to_broadcast([128, nh, 8]), op=ALU.mult),
             r=[kb, "sn2", ktp], w=[ktp])
        S.op("dve", lambda e: e.tensor_tensor(out=rp[:], in0=rp[:], in1=tp, op=ALU.add),
             r=[ktp, krp], w=[krp])

    IN_TILES = ((0, 512), (512, 256), (768, 512), (1280, 512), (1792, 512))

    def front(j):
        s = j % 2
        xt = big[s]
        kx, kss, kvar, krstd, kxn, kxnT = ("big", s), ("ss", s), ("var", s), ("rstd", s), ("xn", s), ("xnT", s)
        dma("sp", xt[:], x_d[j * 128:(j + 1) * 128, :], w=[kx])
        S.op("act", lambda e: e.activation(out=junk[:], in_=xt[:], func=AF.Square, accum_out=st(s)),
             r=[kx], w=[kss, "junk"])
        rstd_from_ss(st(s), kss, D, st(4 + s), krstd, st(2 + s), kvar)
        S.op("dve", lambda e: e.tensor_scalar(out=xn_bf[s][:], in0=xt[:], scalar1=st(4 + s), scalar2=None,
                                              op0=ALU.mult), r=[kx, krstd], w=[kxn])
        transposes8(xn_bf[s], kxn, 0)
        S.op("dve", lambda e: e.tensor_tensor(
            out=xnT[s][:], in0=bank_bf(0).rearrange("p (c t) -> p c t", c=8),
            in1=g1.unsqueeze(2).to_broadcast([128, 8, 128]), op=ALU.mult),
            r=[("bank", 0), "gains"], w=[kxnT])

    def back(j):
        s = j % 2
        r8 = j % 8
        kxnT = ("xnT", s)
        qk = qk_tok[s]
        for nt, (c0, ncol) in enumerate(IN_TILES):
            bk = 1 + nt % 2
            def mm(e, c0=c0, ncol=ncol, bk=bk):
                ins = None
                for c in range(8):
                    ins = e.matmul(bank(bk, ncol), lhsT=xnT[s][:, c, :], rhs=w_in_sb[:, c, c0:c0 + ncol],
                                   start=(c == 0), stop=(c == 7))
                return ins
            S.op("pe", mm, r=[kxnT, "w_in"], w=[("bank", bk)])
            kb = ("bank", bk)
            if nt == 0:
                S.op("act", lambda e, bk=bk: e.activation(
                    out=qk[:, 0:512].rearrange("p (i g d) -> p g i d", i=4, g=2, d=64),
                    in_=bank(bk).rearrange("p (g i d) -> p g i d", g=2, i=4, d=64), func=AF.Copy),
                    r=[kb], w=[("qk", s, 0)])
                rope(bank(bk).rearrange("p (h d) -> p h d", h=8), 8, ropeA[s], ("ropeA", s),
                     ropetmp[:, 0:128].rearrange("p (h d) -> p h d", h=8), "ropetmp", kb, j)
                S.op("dve", lambda e: e.tensor_copy(
                    out=qk[:, 0:512].rearrange("p (i g d) -> p g i d", i=4, g=2, d=64)[:, :, :, 0:16],
                    in_=ropeA[s][:].rearrange("p (g i) d -> p g i d", g=2)),
                    r=[("ropeA", s), ("qk", s, 0)], w=[("qk", s, 0)])
            elif nt == 1:
                S.op("dve", lambda e, bk=bk: e.tensor_copy(out=qk[:, 512:640], in_=bank(bk, 128)),
                     r=[kb], w=[("qk", s, 1)])
                rope(bank(bk, 128).rearrange("p (h d) -> p h d", h=2), 2, ropeK[s], ("ropeK", s),
                     ropetmp[:, 128:160].rearrange("p (h d) -> p h d", h=2), "ropetmpk", kb, j)
                S.op("dve", lambda e: e.tensor_copy(
                    out=qk[:, 512:640].rearrange("p (h d) -> p h d", h=2)[:, :, 0:16], in_=ropeK[s][:]),
                    r=[("ropeK", s), ("qk", s, 1)], w=[("qk", s, 1)])
                S.op("act", lambda e, bk=bk: e.activation(
                    out=vA[:, r8, :, 0:64], in_=bank(bk, 128, 128).rearrange("p (g d) -> p g d", g=2), func=AF.Copy),
                    r=[kb], w=[("vA", r8)])
            elif nt == 2:
                S.op("act", lambda e, bk=bk: e.activation(out=qk[:, 640:1152], in_=bank(bk), func=AF.Copy),
                     r=[kb], w=[("qk", s, 2)])
            elif nt == 3:
                S.op("dve", lambda e, bk=bk: e.tensor_copy(out=qk[:, 1152:1664], in_=bank(bk)),
                     r=[kb], w=[("qk", s, 3)])
            else:
                S.op("act", lambda e, bk=bk: e.activation(
                    out=vB[:, r8, :, 0:64], in_=bank(bk).rearrange("p (h d) -> p h d", h=8), func=AF.Copy),
                    r=[kb], w=[("vB", r8)])
            yield

        def tq(e):
            ins = None
            o = bank_bf(0)
            for b in range(4):
                ins = e.transpose(out=o[:, b * 128:(b + 1) * 128], in_=qk[:, b * 128:(b + 1) * 128], identity=ident_bf[:])
            for b in range(4):
                ins = e.transpose(out=o[:, (4 + b) * 128:(5 + b) * 128], in_=qk[:, 640 + b * 128:640 + (b + 1) * 128],
                                  identity=ident_bf[:])
            return ins
        S.op("pe", tq, r=[("qk", s, 0), ("qk", s, 2), "ident"], w=[("bank", 0)])
        S.op("act", lambda e: e.activation(out=qT[:, :, r8, :], in_=bank_bf(0).rearrange("p (b t) -> p b t", b=8),
                                           func=AF.Copy), r=[("bank", 0)], w=[("qT", r8)])
        yield

        def tk(e):
            ins = None
            o = bank_bf(3)
            ins = e.transpose(out=o[:, 0:128], in_=qk[:, 512:640], identity=ident_bf[:])
            for b in range(4):
                ins = e.transpose(out=o[:, (1 + b) * 128:(2 + b) * 128], in_=qk[:, 1152 + b * 128:1152 + (b + 1) * 128],
                                  identity=ident_bf[:])
            return ins
        S.op("pe", tk, r=[("qk", s, 1), ("qk", s, 3), "ident"], w=[("bank", 3)])
        S.op("dve", lambda e: e.tensor_copy(out=kaT[:, r8, :], in_=bank_bf(3)[:, 0:128]),
             r=[("bank", 3)], w=[("kaT", r8)])
        S.op("dve", lambda e: e.tensor_copy(out=kbT[:, :, r8, :],
                                            in_=bank_bf(3)[:, 128:640].rearrange("p (b t) -> p b t", b=4)),
             r=[("bank", 3)], w=[("kbT", r8)])
        yield

    def load_special(idx):
        for h in range(8):
            bs = bstage[0]
            dma("sp", bs[:], biasS_d[idx, :, h * 768:(h + 1) * 768], w=[("bstage", 0)])
            S.op("act", (lambda h, bs: lambda e: e.activation(
                out=expbS[:, h, :, :].rearrange("p j q -> p (j q)"), in_=bs[:], func=AF.Exp))(h, bs),
                r=[("bstage", 0)], w=[("expbS", h)])

    def attention(e_, filler=None, after_a=None):
        j = e_ + 2
        rq = j % 8
        so = 2 + e_ % 2
        o_tok = big[so]
        ko = ("big", so)
        for g in range(2):
            for di, dj in enumerate((-1, 0, 1)):
                rk = (j + dj) % 8
                bk = 3 + di % 2
                S.op("pe", lambda e, g=g, rk=rk, bk=bk: e.matmul(
                    bank(bk), lhsT=kaT[64 * g:64 * g + 64, rk, :], rhs=qT[64 * g:64 * g + 64, 0:4, rq, :],
                    start=True, stop=True), r=[("kaT", rk), ("qT", rq)], w=[("bank", bk)])
                S.op("act", lambda e, g=g, di=di, bk=bk: e.activation(
                    out=pA[g][:, di, :], in_=bank(bk), func=AF.Exp, scale=0.125),
                    r=[("bank", bk)], w=[("pA", g, di)])
                if dj != 0:
                    mi = (2 if e_ == 1 else 0) if dj < 0 else (3 if e_ == NO else 1)
                    S.op("dve", lambda e, g=g, di=di, mi=mi: e.tensor_tensor(
                        out=pA[g][:, di, :].rearrange("p (i q) -> p i q", i=4),
                        in0=pA[g][:, di, :].rearrange("p (i q) -> p i q", i=4),
                        in1=maskA_bf[:, mi:mi + 1, :].to_broadcast([128, 4, 128]), op=ALU.mult),
                        r=[("pA", g, di), "maskA"], w=[("pA", g, di)])

            def pv(e, g=g):
                ins = None
                for i in range(4):
                    for di, dj in enumerate((-1, 0, 1)):
                        rk = (j + dj) % 8
                        ins = e.matmul(bank(5 + g, 65, i * 65), lhsT=pA[g][:, di, i * 128:(i + 1) * 128],
                                       rhs=vA[:, rk, g, :], start=(di == 0), stop=(di == 2))
                return ins
            S.op("pe", pv, r=[("pA", g, 0), ("pA", g, 1), ("pA", g, 2)] + [("vA", (j + dj) % 8) for dj in (-1, 0, 1)],
                 w=[("bank", 5 + g)])
            S.op("dve", lambda e, g=g: e.tensor_tensor(
                out=den[:, 4 * g:4 * g + 4], in0=bank(5 + g, 260).rearrange("p (i d) -> p i d", i=4)[:, :, 64],
                in1=esink[:, 4 * g:4 * g + 4], op=ALU.add), r=[("bank", 5 + g), "esink"], w=[("denA", g)])
            S.op("dve", lambda e, g=g: e.reciprocal(out=den[:, 8 + 4 * g:12 + 4 * g], in_=den[:, 4 * g:4 * g + 4]),
                 r=[("denA", g)], w=[("rdenA", g)])
            S.op("dve", lambda e, g=g: e.tensor_tensor(
                out=o_tok[:, 256 * g:256 * g + 256].rearrange("p (i d) -> p i d", i=4),
                in0=bank(5 + g, 260).rearrange("p (i d) -> p i d", i=4)[:, :, 0:64],
                in1=den[:, 8 + 4 * g:12 + 4 * g].unsqueeze(2).to_broadcast([128, 4, 64]), op=ALU.mult),
                r=[("bank", 5 + g), ("rdenA", g)], w=[ko])
        if after_a is not None:
            after_a()
        if e_ in SPECIAL_E:
            djs = DJ_TOP if e_ in (1, 2) else DJ_BOT
            tab = expbS
            tkey = "expbS"
        else:
            djs = DJ_GEN
            tab = expbG
            tkey = "expbG"
        nd = len(djs)

        def sT(h):
            b, hp = h // 2, h % 2
            def fn(e):
                ins = None
                for di, dj in enumerate(djs):
                    rk = (j + dj) % 8
                    ins = e.matmul(psum[:, 3 * 512 + di * 128:3 * 512 + (di + 1) * 128],
                                   lhsT=kbT[64 * hp:64 * hp + 64, b, rk, :], rhs=qT[64 * hp:64 * hp + 64, 4 + b, rq, :],
                                   start=True, stop=True)
                return ins
            S.op("pe", fn, r=[("kbT", (j + dj) % 8) for dj in djs] + [("qT", rq)], w=[("bank", 3), ("bank", 4)])
            S.op("act", lambda e: e.activation(out=pB[hp][:, 0:nd * 128], in_=psum[:, 3 * 512:3 * 512 + nd * 128],
                                               func=AF.Exp, scale=0.125),
                 r=[("bank", 3), ("bank", 4)], w=[("pB", hp)])
            S.op("dve", lambda e: e.tensor_tensor(
                out=pB[hp][:, 0:nd * 128], in0=pB[hp][:, 0:nd * 128],
                in1=tab[:, h, 0:nd, :].rearrange("p j q -> p (j q)"), op=ALU.mult),
                r=[("pB", hp), (tkey, h)], w=[("pB", hp)])

        def pvB(h):
            hp = h % 2
            def fn(e):
                ins = None
                for di, dj in enumerate(djs):
                    rk = (j + dj) % 8
                    ins = e.matmul(bank(5 + h // 4, 65, (h % 4) * 65), lhsT=pB[hp][:, di * 128:(di + 1) * 128],
                                   rhs=vB[:, rk, h, :], start=(di == 0), stop=(di == nd - 1))
                return ins
            S.op("pe", fn, r=[("pB", hp)] + [("vB", (j + dj) % 8) for dj in djs], w=[("bank", 5 + h // 4)])

        for h in range(9):
            if h < 8:
                sT(h)
                if filler is not None:
                    next(filler, None)
            if h >= 1:
                pvB(h - 1)
        for g in range(2):
            S.op("dve", lambda e, g=g: e.reciprocal(
                out=den[:, 16 + 4 * g:20 + 4 * g], in_=bank(5 + g, 260).rearrange("p (i d) -> p i d", i=4)[:, :, 64]),
                r=[("bank", 5 + g)], w=[("rdenB", g)])
            S.op("dve", lambda e, g=g: e.tensor_tensor(
                out=o_tok[:, 512 + 256 * g:768 + 256 * g].rearrange("p (i d) -> p i d", i=4),
                in0=bank(5 + g, 260).rearrange("p (i d) -> p i d", i=4)[:, :, 0:64],
                in1=den[:, 16 + 4 * g:20 + 4 * g].unsqueeze(2).to_broadcast([128, 4, 64]), op=ALU.mult),
                r=[("bank", 5 + g), ("rdenB", g)], w=[ko])
        for half in range(2):
            S.op("act", lambda e, half=half: e.activation(
                out=junk[:, 0:512], in_=o_tok[:, 512 * half:512 * half + 512], func=AF.Square,
                accum_out=st(8 + half)), r=[ko], w=[("ssm", half), "junk"])
            rstd_from_ss(st(8 + half), ("ssm", half), 512, st(12 + half), ("rstdm", half), st(10 + half), ("varm", half))
            S.op("dve", lambda e, half=half: e.tensor_scalar(
                out=mixn[:, 512 * half:512 * half + 512],
                in0=o_tok[:, 512 * half:512 * half + 512], scalar1=st(12 + half), scalar2=None, op0=ALU.mult),
                r=[ko, ("rstdm", half)], w=["mixn"])

    def outproj(e_):
        j = e_ + 2
        transposes8(mixn, "mixn", 0)
        S.op("dve", lambda e: e.tensor_tensor(
            out=mixT[:], in0=bank_bf(0).rearrange("p (c t) -> p c t", c=8),
            in1=gab.unsqueeze(2).to_broadcast([128, 8, 128]), op=ALU.mult),
            r=[("bank", 0), "gains"], w=["mixT"])
        h1t = big[4]
        kh = ("big", 4)
        dma("sp", h1t[:], x_d[j * 128:(j + 1) * 128, :], w=[kh])
        for n in range(2):
            bk = 1 + n
            def mm(e, n=n, bk=bk):
                ins = None
                for c in range(8):
                    ins = e.matmul(bank(bk), lhsT=mixT[:, c, :], rhs=w_out_sb[:, c, n * 512:(n + 1) * 512],
                                   start=(c == 0), stop=(c == 7))
                return ins
            S.op("pe", mm, r=["mixT", "w_out"], w=[("bank", bk)])
            S.op("dve", lambda e, n=n, bk=bk: e.tensor_tensor(
                out=h1t[:, n * 512:(n + 1) * 512], in0=bank(bk), in1=h1t[:, n * 512:(n + 1) * 512], op=ALU.add),
                r=[("bank", bk), kh], w=[kh])
        dma("sp", h1d[e_ * 128:(e_ + 1) * 128, :], h1t[:], r=[kh], w=[("h1d", e_)])

    mixn = sb("mixn", [128, 1024], BF16)

    LOOK = 6
    spec_idx = {1: 0, 2: 1, NO - 1: 2, NO: 3}
    front(0)
    pending_tail = [None]
    for t in range(NX + LOOK):
        if t + 1 < NX:
            front(t + 1)
        filler = back(t) if t < NX else None
        e_ = t - LOOK
        if 0 <= e_ < NE:
            if e_ in spec_idx:
                load_special(spec_idx[e_])
            prev = pending_tail[0]
            attention(e_, filler=filler,
                      after_a=(lambda prev=prev: outproj(prev)) if prev is not None else None)
            pending_tail[0] = e_
        if filler is not None:
            for _ in filler:
                pass
        if t == NX - 1:
            for c in range(6):
                dma("pool", wg_sb[:, c, :], w_gate_d[c * 128:(c + 1) * 128, :], w=["wg", "w_in"])
    outproj(pending_tail[0])

    bar = S.barrier_events()
    first = True
    for c in range(6, 8):
        dma("pool", wg_sb[:, c, :], w_gate_d[c * 128:(c + 1) * 128, :], w=["wg"], extra=bar if first else ())
        first = False
    for c in range(8):
        dma("pool", wu_sb[:, c, :], w_up_d[c * 128:(c + 1) * 128, :], w=["wu"])
    for fc in range(NFC):
        dma("pool", wd_sb[:, fc, :], w_down_d[fc * 128:(fc + 1) * 128, :], w=[("wd", fc)])

    def prep2(e_):
        rh = e_ % 5
        h1r = big[rh]
        kh = ("big", rh)
        sl = (e_ - 1) % 8
        dma("sp", h1r[:], h1d[e_ * 128:(e_ + 1) * 128, :], r=[("h1d", e_)], w=[kh])
        S.op("act", lambda e: e.activation(out=junk[:], in_=h1r[:], func=AF.Square, accum_out=st(16)),
             r=[kh], w=["ss2", "junk"])
        rstd_from_ss(st(16), "ss2", D, st(18), "rstd2", st(17), "var2")
        if e_ in (0, NE - 1):
            hv = hvalid[:, 0:1] if e_ == 0 else hvalid[:, 1:2]
            S.op("dve", lambda e: e.tensor_tensor(out=st(18), in0=st(18), in1=hv, op=ALU.mult),
                 r=["rstd2", "hvalid"], w=["rstd2"])
        S.op("dve", lambda e: e.tensor_scalar(out=mixn[:], in0=h1r[:], scalar1=st(18), scalar2=None, op0=ALU.mult),
             r=[kh, "rstd2"], w=["mixn"])
        transposes8(mixn, "mixn", 0)
        S.op("dve", lambda e: e.tensor_tensor(
            out=hn2T[:, :, sl, :], in0=bank_bf(0).rearrange("p (c t) -> p c t", c=8),
            in1=g2.unsqueeze(2).to_broadcast([128, 8, 128]), op=ALU.mult),
            r=[("bank", 0), "gains"], w=[("hn2T", sl)])

    def ffn(s_):
        ea = 2 * s_ + 1
        sa = (2 * s_) % 8
        sl_l = (2 * s_ - 1) % 8
        sl_r = (2 * s_ + 2) % 8
        rkeys = [("hn2T", sa), ("hn2T", sa + 1), ("hn2T", sl_l), ("hn2T", sl_r)]

        wi = win[0]
        kw = ("win", 0)
        S.op("pool", lambda e: e.tensor_copy(out=wi[:, :, 1:257].rearrange("p c (a t) -> p c a t", a=2),
                                             in_=hn2T[:, :, sa:sa + 2, :]),
             r=[("hn2T", sa), ("hn2T", sa + 1)], w=[kw])
        S.op("pool", lambda e: e.tensor_copy(out=wi[:, :, 0:1], in_=hn2T[:, :, sl_l, 127:128]),
             r=[("hn2T", sl_l), kw], w=[kw])
        S.op("pool", lambda e: e.tensor_copy(out=wi[:, :, 257:258], in_=hn2T[:, :, sl_r, 0:1]),
             r=[("hn2T", sl_r), kw], w=[kw])

        def gu(fc):
            pb = fc % 2
            bk = 1 + pb
            ub = 3 if pb == 0 else 0
            def fn(e):
                ins = None
                fs = slice(fc * 128, (fc + 1) * 128)
                for c in range(8):
                    ins = e.matmul(bank(bk, 258), lhsT=wg_sb[:, c, fs], rhs=wi[:, c, 0:258],
                                   start=(c == 0), stop=(c == 7))
                for c in range(8):
                    ins = e.matmul(bank(ub, 256), lhsT=wu_sb[:, c, fs], rhs=wi[:, c, 1:257],
                                   start=(c == 0), stop=(c == 7))
                return ins
            S.op("pe", fn, r=[kw, "wg", "wu"], w=[("bank", bk), ("bank", ub)])

        def chain(fc):
            pb = fc % 2
            bk = 1 + pb
            ub = 3 if pb == 0 else 0
            acc = accs[pb]
            sg = sgs[pb]
            kb = ("bank", bk)
            gps = bank(bk, 258)
            S.op("act", lambda e: e.activation(out=usb[pb][:], in_=bank(ub, 256), func=AF.Copy),
                 r=[("bank", ub)], w=[("usb", pb)])
            S.op("act", lambda e: e.activation(out=acc[:], in_=gps[:, 1:257], func=AF.Identity,
                                               bias=cb[:, fc:fc + 1], scale=cw[:, fc, 1:2]),
                 r=[kb, "cw", "cb"], w=[("acc", pb)])
            S.op("dve", lambda e: e.scalar_tensor_tensor(out=acc[:], in0=gps[:, 0:256], scalar=cw[:, fc, 0:1],
                                                         in1=acc[:], op0=ALU.mult, op1=ALU.add),
                 r=[kb, ("acc", pb)], w=[("acc", pb)])
            S.op("dve", lambda e: e.scalar_tensor_tensor(out=acc[:], in0=gps[:, 2:258], scalar=cw[:, fc, 2:3],
                                                         in1=acc[:], op0=ALU.mult, op1=ALU.add),
                 r=[kb, ("acc", pb)], w=[("acc", pb)])
            S.op("act", lambda e: e.activation(out=sg[:], in_=acc[:], func=AF.Silu),
                 r=[("acc", pb)], w=[("sg", pb)])
            S.op("pool", lambda e: e.tensor_tensor(out=actc[pb][:], in0=sg[:], in1=usb[pb][:], op=ALU.mult),
                 r=[("sg", pb), ("usb", pb)], w=[("actc", pb)])

        def down(fc):
            pb = fc % 2
            def fn(e):
                ins = None
                for tt in range(2):
                    for n in range(2):
                        ins = e.matmul(bank(4 + tt * 2 + n), lhsT=actc[pb][:, tt * 128:(tt + 1) * 128],
                                       rhs=wd_sb[:, fc, n * 512:(n + 1) * 512], start=(fc == 0), stop=(fc == NFC - 1))
                return ins
            S.op("pe", fn, r=[("actc", pb), ("wd", fc)], w=[("bank", 4), ("bank", 5), ("bank", 6), ("bank", 7)])

        gu(0)
        gu(1)
        for fc in range(NFC):
            chain(fc)
            if fc + 2 < NFC:
                gu(fc + 2)
            down(fc)
        for tt in range(2):
            e_ = ea + tt
            rh = e_ % 5
            h1r = big[rh]
            kh = ("big", rh)
            for n in range(2):
                S.op("dve", lambda e, tt=tt, n=n, h1r=h1r: e.tensor_tensor(
                    out=h1r[:, n * 512:(n + 1) * 512], in0=bank(4 + tt * 2 + n), in1=h1r[:, n * 512:(n + 1) * 512],
                    op=ALU.add), r=[("bank", 4 + tt * 2 + n), kh], w=[kh])
            S.op("act", lambda e, h1r=h1r, tt=tt: e.activation(out=junk[:], in_=h1r[:], func=AF.Square,
                                                               accum_out=st(20 + tt)), r=[kh], w=[("ss3", tt), "junk"])
            rstd_from_ss(st(20 + tt), ("ss3", tt), D, st(24 + tt), ("rstd3", tt), st(22 + tt), ("var3", tt))
            S.op("dve", lambda e, h1r=h1r, tt=tt: e.scalar_tensor_tensor(
                out=h1r[:], in0=h1r[:], scalar=st(24 + tt), in1=gfin[:], op0=ALU.mult, op1=ALU.mult),
                r=[kh, ("rstd3", tt), "gfin"], w=[kh])
            dma("sp", out_d[(e_ - 1) * 128:e_ * 128, :], h1r[:], r=[kh], w=[("out", e_)], is_out=True)

    for e_ in range(NE):
        prep2(e_)
        if e_ >= 3 and (e_ - 3) % 2 == 0:
            ffn((e_ - 3) // 2)

    def semh(sk):
        return sem_dma[sk[1]] if isinstance(sk, tuple) else sem_eng[sk]

    def emit(eng_name, e, final_waits=()):
        for waits, fn, ev in S.ops[eng_name]:
            for sk, v in waits:
                e.wait_ge(semh(sk), v)
            ins = fn(e)
            if isinstance(ev[0], tuple):
                ins.then_inc(semh(ev[0]), 16)
            else:
                ins.then_inc(semh(ev[0]), 1)
        for sk, v in final_waits:
            e.wait_ge(semh(sk), v)

    fin = {}
    for ev in S.out_events:
        fin[ev[0]] = max(fin.get(ev[0], 0), ev[1])

    with nc.Block() as block:
        @block.sync
        def _(e):
            emit("sp", e, final_waits=list(fin.items()))

        @block.tensor
        def _(e):
            emit("pe", e)

        @block.scalar
        def _(e):
            emit("act", e)

        @block.vector
        def _(e):
            emit("dve", e)

        @block.gpsimd
        def _(e):
            emit("pool", e)
    es.close()
    return nc


def _nbr_table(rpb, m, djs):
    rows = SEQ // 64
    nd = len(djs)
    tab = np.full((128, 8, nd, 128), MASKV, dtype=np.float32)
    qr = np.repeat(np.arange(2), 64) + 2 * m
    qc = np.tile(np.arange(64), 2)
    rs = np.clip(qr - 4, 0, rows - 8)
    cs = np.clip(qc - 8, 0, 64 - 16)
    for di, dj in enumerate(djs):
        mk = m + dj
        if mk < 0 or mk >= rows // 2:
            continue
        kr = np.repeat(np.arange(2), 64) + 2 * mk
        kc = np.tile(np.arange(64), 2)
        valid = ((kr[:, None] >= rs[None, :]) & (kr[:, None] < rs[None, :] + 8) &
                 (kc[:, None] >= cs[None, :]) & (kc[:, None] < cs[None, :] + 16))
        ro = np.clip(kr[:, None] - qr[None, :] + 7, 0, 14)
        co = np.clip(kc[:, None] - qc[None, :] + 15, 0, 30)
        for h in range(8):
            g = rpb[h][ro, co]
            tab[:, h, di, :] = np.where(valid, g, np.float32(MASKV))
    return tab


_NC_CACHE = {}


def _shared_inputs(norm_mix, w_in, sink, rpb, norm_a, norm_b, w_out, norm_ffn,
                   w_gate, w_up, conv_w, conv_b, w_down, norm_final):
    f32 = np.float32

    def fm(v, nchunk):
        return np.ascontiguousarray(np.asarray(v, f32).reshape(nchunk, 128).T)

    gains = np.concatenate([fm(np.asarray(norm_mix)[0], 8), fm(np.asarray(norm_ffn)[0], 8),
                            fm(np.concatenate([np.asarray(norm_a)[0], np.asarray(norm_b)[0]]), 8)], axis=1)
    cwv = np.asarray(conv_w, f32)[0]
    rpb0 = np.asarray(rpb, f32)[0]
    return {
        "w_in": np.ascontiguousarray(np.asarray(w_in, f32)[0]),
        "w_out": np.ascontiguousarray(np.asarray(w_out, f32)[0]),
        "w_gate": np.ascontiguousarray(np.asarray(w_gate, f32)[0]),
        "w_up": np.ascontiguousarray(np.asarray(w_up, f32)[0]),
        "w_down": np.ascontiguousarray(np.asarray(w_down, f32)[0]),
        "ident": np.eye(128, dtype=f32),
        "gains": np.ascontiguousarray(gains, dtype=f32),
        "gfin": np.ascontiguousarray(np.asarray(norm_final, f32)),
        "biasG": np.ascontiguousarray(_nbr_table(rpb0, 10, DJ_GEN).reshape(128, 8 * 5 * 128)),
        "sinkb": np.ascontiguousarray(np.broadcast_to(np.asarray(sink, f32)[0][None, :], (128, 8))),
        "cw": np.ascontiguousarray(np.stack([fm(cwv[k], NFC) for k in range(3)], axis=2).reshape(128, NFC * 3)),
        "cb": fm(np.asarray(conv_b, f32)[0], NFC),
    }, rpb0


def _core_inputs(xb, own_start, NO, rpb0):
    f32 = np.float32
    NX = NO + 6
    t0 = own_start - 384
    xe = np.zeros((NX * 128, D), f32)
    lo, hi = max(t0, 0), min(t0 + NX * 128, SEQ)
    xe[lo - t0:hi - t0] = xb[lo:hi]
    inv_freq = (np.float32(500000.0) ** (-np.arange(0, 16, 2, dtype=f32) / np.float32(16))).astype(f32)
    pos = (t0 + np.arange(NX * 128)).astype(f32)
    ang = (pos[:, None] * inv_freq[None, :]).astype(f32)
    cosv, sinv = np.cos(ang).astype(f32), np.sin(ang).astype(f32)
    cs2 = np.concatenate([cosv, cosv], axis=1).reshape(NX, 128, 16).transpose(1, 0, 2)
    sn2 = np.concatenate([-sinv, sinv], axis=1).reshape(NX, 128, 16).transpose(1, 0, 2)
    kk = np.arange(128)[:, None]
    qq = np.arange(128)[None, :]
    m_prev = (kk >= qq).astype(f32)
    m_next = (kk <= qq).astype(f32)
    zeros = np.zeros((128, 128), f32)
    at_start = own_start == 0
    at_end = own_start + NO * 128 == SEQ
    maskA = np.stack([m_prev, m_next, zeros if at_start else m_prev, zeros if at_end else m_next], axis=1)
    mbase = own_start // 128
    spec = []
    for e_, djs in ((1, DJ_TOP), (2, DJ_TOP), (NO - 1, DJ_BOT), (NO, DJ_BOT)):
        spec.append(_nbr_table(rpb0, mbase + e_ - 1, djs).reshape(128, 8 * 6 * 128))
    hvalid = np.empty((128, 2), f32)
    hvalid[:, 0] = 0.0 if at_start else 1.0
    hvalid[:, 1] = 0.0 if at_end else 1.0
    return {
        "x": xe,
        "cs2": np.ascontiguousarray(cs2.reshape(128, NX * 16)),
        "sn2": np.ascontiguousarray(sn2.reshape(128, NX * 16)),
        "maskA": np.ascontiguousarray(maskA.reshape(128, 4 * 128)),
        "biasS": np.ascontiguousarray(np.stack(spec, axis=0)),
        "hvalid": hvalid,
    }


def kernel(x, norm_mix, w_in, sink, rpb, norm_a, norm_b, w_out, norm_ffn,
           w_gate, w_up, conv_w, conv_b, w_down, norm_final):
    f32 = np.float32
    x = np.asarray(x, f32)
    shared, rpb0 = _shared_inputs(norm_mix, w_in, sink, rpb, norm_a, norm_b, w_out, norm_ffn,
                                  w_gate, w_up, conv_w, conv_b, w_down, norm_final)
    NO = NO_FULL
    OWN = NO * 128
    in_maps = []
    for c in range(NCORES):
        b, half = c // 2, c % 2
        m = dict(shared)
        m.update(_core_inputs(x[b], half * OWN, NO, rpb0))
        in_maps.append(m)
    if "nc" not in _NC_CACHE:
        _NC_CACHE["nc"] = build_program(NO)
    res = run_bass_kernel_spmd(_NC_CACHE["nc"], in_maps, core_ids=list(range(NCORES)))
    out = np.empty((4, SEQ, D), f32)
    for c in range(NCORES):
        b, half = c // 2, c % 2
        out[b, half * OWN:(half + 1) * OWN] = res.results[c]["out"]
    return out
```
